# Optimizing a Trainium2 kernel written in Bass

```python
import jax
import jax.numpy as jnp
from jax import lax
import numpy as np

D_MODEL = 1024
BATCH = 4
SEQ = 4096
DEPTH = 4

CHUNK = 64
LEFT_CHUNKS = 8
BAND = (LEFT_CHUNKS + 1) * CHUNK
N_AB_LAYERS = (DEPTH + 1) // 2
N_C_LAYERS = DEPTH // 2
A_HEADS = 8
A_HEAD_DIM = 64
A_WIDTH = A_HEADS * A_HEAD_DIM
REL_CLIP = 128
B_HEADS = 8
B_HEAD_SIZE = 64
B_WIDTH = B_HEADS * B_HEAD_SIZE
W_LORA = 64
A_LORA = 64
G_LORA = 128
V_LORA = 32
DECAY_SCALE = 0.606531
GN_EPS = 64e-5
B_COLS = 3 * B_WIDTH + W_LORA + A_LORA + G_LORA
B_SPLITS = (B_WIDTH, 2 * B_WIDTH, 3 * B_WIDTH, 3 * B_WIDTH + W_LORA, 3 * B_WIDTH + W_LORA + A_LORA)
AB_IN_COLS = 3 * A_WIDTH + B_COLS
AB_MIX_WIDTH = A_WIDTH + B_WIDTH
D_RNN = 1280
C_BLOCKS = 10
C_BLOCK_W = D_RNN // C_BLOCKS
CONV_W = 4
RG_C = 8.0
D_FF = 2816
NORM_EPS = 1e-6
NEG_INF = -1e30

kernel_name = 'hybrid_chunked_attn_rwkv7_rglru_macaron'


def rmsnorm(x, g):
    xf = x.astype(jnp.float32)
    y = xf * lax.rsqrt(jnp.mean(xf * xf, axis=-1, keepdims=True) + NORM_EPS)
    return (y * g.astype(jnp.float32)).astype(x.dtype)


def swiglu(h, w_gate, w_up, w_down):
    return (jax.nn.silu(h @ w_gate) * (h @ w_up)) @ w_down


def chunk_attention(q, k, v, rel_bias):
    bsz, seq = q.shape[0], q.shape[1]
    nc = seq // CHUNK
    shp = (bsz, nc, CHUNK, A_HEADS, A_HEAD_DIM)
    qc = q.reshape(shp) * (A_HEAD_DIM ** -0.5)
    pad = ((0, 0), (LEFT_CHUNKS, 0), (0, 0), (0, 0), (0, 0))
    kp = jnp.pad(k.reshape(shp), pad)
    vp = jnp.pad(v.reshape(shp), pad)
    kb = jnp.concatenate([kp[:, j:j + nc] for j in range(LEFT_CHUNKS + 1)], axis=2)
    vb = jnp.concatenate([vp[:, j:j + nc] for j in range(LEFT_CHUNKS + 1)], axis=2)
    s = jnp.einsum('bnqhd,bnkhd->bhnqk', qc, kb).astype(jnp.float32)
    dist = np.arange(CHUNK)[:, None] + LEFT_CHUNKS * CHUNK - np.arange(BAND)[None, :]
    idx = np.clip(dist, -REL_CLIP, REL_CLIP) + REL_CLIP
    s = s + rel_bias[:, idx].astype(jnp.float32)[None, :, None]
    key_chunk = np.arange(nc)[:, None] - LEFT_CHUNKS + (np.arange(BAND) // CHUNK)[None, :]
    s = jnp.where((key_chunk >= 0)[None, None, :, None, :], s, NEG_INF)
    p = jax.nn.softmax(s, axis=-1).astype(v.dtype)
    o = jnp.einsum('bhnqk,bnkhd->bnqhd', p, vb)
    return o.reshape(bsz, seq, A_WIDTH)


def _heads(t):
    return t.reshape(t.shape[0], t.shape[1], B_HEADS, B_HEAD_SIZE)


def rwkv7_scan(r, decay, k, v, kk, kka):
    def step(state, inp):
        r_t, w_t, k_t, v_t, kk_t, kka_t = inp
        sa = jnp.einsum('bhvk,bhk->bhv', state, kk_t)
        state = (state * w_t[:, :, None, :] - sa[..., None] * kka_t[:, :, None, :]
                 + v_t[..., None] * k_t[:, :, None, :])
        return state, jnp.einsum('bhvk,bhk->bhv', state, r_t)
    xs = tuple(jnp.moveaxis(t, 1, 0) for t in (r, decay, k, v, kk, kka))
    s0 = jnp.zeros((r.shape[0], B_HEADS, B_HEAD_SIZE, B_HEAD_SIZE), jnp.float32)
    _, o = lax.scan(step, s0, xs)
    return jnp.moveaxis(o, 0, 1)


def rwkv7_group(p, mu, w0, w_up, a0, a_up, g_up, k_k, k_a, r_k, ln_w, ln_b, v_first, v_mix):
    f32 = jnp.float32
    prev = jnp.pad(p, ((0, 0), (1, 0), (0, 0)))[:, :-1]
    p = p + (prev - p) * mu
    r, k, v, wd, ad, gd = jnp.split(p, B_SPLITS, axis=-1)
    v_raw = v
    if v_mix is not None:
        v0, v_down, v_up = v_mix
        v = v + (v_first - v) * jax.nn.sigmoid(v0 + (v @ v_down) @ v_up)
    decay = jnp.exp(-DECAY_SCALE * jax.nn.sigmoid((w0 + jnp.tanh(wd) @ w_up).astype(f32)))
    a = jax.nn.sigmoid(a0 + ad @ a_up).astype(f32)
    g = jax.nn.sigmoid(gd) @ g_up
    kk = _heads((k * k_k).astype(f32))
    kk = kk * lax.rsqrt(jnp.maximum(jnp.sum(kk * kk, axis=-1, keepdims=True), 1e-24))
    k = k.astype(f32) * (1.0 + (a - 1.0) * k_a.astype(f32))
    rh, kh, vh, ah = _heads(r.astype(f32)), _heads(k), _heads(v.astype(f32)), _heads(a)
    o = rwkv7_scan(rh, _heads(decay), kh, vh, kk, kk * ah)
    mean = jnp.mean(o, axis=-1, keepdims=True)
    var = jnp.mean(jnp.square(o - mean), axis=-1, keepdims=True)
    o = (o - mean) * lax.rsqrt(var + GN_EPS)
    o = (o * ln_w.astype(f32).reshape(B_HEADS, B_HEAD_SIZE)
         + ln_b.astype(f32).reshape(B_HEADS, B_HEAD_SIZE))
    o = o + jnp.sum(rh * kh * r_k.astype(f32), axis=-1, keepdims=True) * vh
    o = o.reshape(o.shape[0], o.shape[1], B_WIDTH) * g.astype(f32)
    return o.astype(p.dtype), v_raw


def _lin_combine(c1, c2):
    a1, b1 = c1
    a2, b2 = c2
    return a1 * a2, a2 * b1 + b2


def rglru_block(h, w_in, conv_w, conv_b, wa, ba, wx, bx, lam, w_out):
    f32 = jnp.float32
    gate, xb = jnp.split(h @ w_in, 2, axis=-1)
    gate = jax.nn.gelu(gate)
    xc = lax.conv_general_dilated(xb, conv_w[:, None, :], window_strides=(1,),
                                  padding=[(CONV_W - 1, 0)],
                                  dimension_numbers=('NWC', 'WIO', 'NWC'),
                                  feature_group_count=D_RNN) + conv_b
    xblk = xc.reshape(xc.shape[0], xc.shape[1], C_BLOCKS, C_BLOCK_W)
    r = jax.nn.sigmoid((jnp.einsum('bsnc,ncd->bsnd', xblk, wa).reshape(xc.shape) + ba).astype(f32))
    ig = jax.nn.sigmoid((jnp.einsum('bsnc,ncd->bsnd', xblk, wx).reshape(xc.shape) + bx).astype(f32))
    log_a = -RG_C * r * jax.nn.softplus(-lam.astype(f32))
    a = jnp.exp(log_a)
    mult = jnp.sqrt(jnp.maximum(-jnp.expm1(2.0 * log_a), 0.0))
    b = mult * ig * xc.astype(f32)
    _, hs = lax.associative_scan(_lin_combine, (a, b), axis=1)
    return (hs.astype(h.dtype) * gate) @ w_out


def setup_inputs(seed: int = 0) -> dict:
    key = jax.random.key(seed)
    ks = iter(jax.random.split(key, 48))
    f32 = jnp.float32

    def nrm(shape, scale):
        return jax.random.normal(next(ks), shape, f32) * scale

    def uni(shape, lo, hi):
        return jax.random.uniform(next(ks), shape, f32, lo, hi)

    na, nc = N_AB_LAYERS, N_C_LAYERS
    u = uni((nc, D_RNN), 0.9, 0.999)
    sig = u ** (1.0 / RG_C)
    lam = jnp.log(sig) - jnp.log1p(-sig)
    return {
        'x': nrm((BATCH, SEQ, D_MODEL), 1.0),
        'norm_ffn': 1.0 + nrm((DEPTH, 2, D_MODEL), 0.02),
        'ffn_w_gate': nrm((DEPTH, 2, D_MODEL, D_FF), D_MODEL ** -0.5),
        'ffn_w_up': nrm((DEPTH, 2, D_MODEL, D_FF), D_MODEL ** -0.5),
        'ffn_w_down': nrm((DEPTH, 2, D_FF, D_MODEL), D_FF ** -0.5),
        'norm_mix': 1.0 + nrm((DEPTH, D_MODEL), 0.02),
        'ab_w_in': nrm((na, D_MODEL, AB_IN_COLS), D_MODEL ** -0.5),
        'ab_w_out': nrm((na, AB_MIX_WIDTH, D_MODEL), AB_MIX_WIDTH ** -0.5),
        'att_rel_bias': nrm((na, A_HEADS, 2 * REL_CLIP + 1), 0.5),
        'rwkv_mu': uni((na, B_COLS), 0.0, 1.0),
        'rwkv_w0': uni((na, B_WIDTH), -5.0, 1.0),
        'rwkv_w_up': nrm((na, W_LORA, B_WIDTH), W_LORA ** -0.5),
        'rwkv_a0': nrm((na, B_WIDTH), 0.1),
        'rwkv_a_up': nrm((na, A_LORA, B_WIDTH), A_LORA ** -0.5),
        'rwkv_g_up': nrm((na, G_LORA, B_WIDTH), G_LORA ** -0.5),
        'rwkv_k_k': 0.85 + nrm((na, B_WIDTH), 0.02),
        'rwkv_k_a': 1.0 + nrm((na, B_WIDTH), 0.02),
        'rwkv_r_k': nrm((na, B_HEADS, B_HEAD_SIZE), 0.1),
        'rwkv_ln_w': 1.0 + nrm((na, B_WIDTH), 0.02),
        'rwkv_ln_b': nrm((na, B_WIDTH), 0.02),
        'rwkv_v0': nrm((na - 1, B_WIDTH), 0.1),
        'rwkv_v_down': nrm((na - 1, B_WIDTH, V_LORA), B_WIDTH ** -0.5),
        'rwkv_v_up': nrm((na - 1, V_LORA, B_WIDTH), V_LORA ** -0.5),
        'c_w_in': nrm((nc, D_MODEL, 2 * D_RNN), D_MODEL ** -0.5),
        'c_conv_w': nrm((nc, CONV_W, D_RNN), CONV_W ** -0.5),
        'c_conv_b': nrm((nc, D_RNN), 0.02),
        'c_wa': nrm((nc, C_BLOCKS, C_BLOCK_W, C_BLOCK_W), C_BLOCK_W ** -0.5),
        'c_ba': nrm((nc, D_RNN), 0.02),
        'c_wx': nrm((nc, C_BLOCKS, C_BLOCK_W, C_BLOCK_W), C_BLOCK_W ** -0.5),
        'c_bx': nrm((nc, D_RNN), 0.02),
        'c_lambda': lam,
        'c_w_out': nrm((nc, D_RNN, D_MODEL), D_RNN ** -0.5),
        'norm_final': 1.0 + nrm((D_MODEL,), 0.02),
    }


def reference(x, norm_ffn, ffn_w_gate, ffn_w_up, ffn_w_down, norm_mix, ab_w_in, ab_w_out,
              att_rel_bias, rwkv_mu, rwkv_w0, rwkv_w_up, rwkv_a0, rwkv_a_up, rwkv_g_up,
              rwkv_k_k, rwkv_k_a, rwkv_r_k, rwkv_ln_w, rwkv_ln_b, rwkv_v0, rwkv_v_down,
              rwkv_v_up, c_w_in, c_conv_w, c_conv_b, c_wa, c_ba, c_wx, c_bx, c_lambda,
              c_w_out, norm_final):
    v_first = None
    for l in range(DEPTH):
        x = x + 0.5 * swiglu(rmsnorm(x, norm_ffn[l, 0]), ffn_w_gate[l, 0], ffn_w_up[l, 0], ffn_w_down[l, 0])
        h = rmsnorm(x, norm_mix[l])
        if l % 2 == 0:
            i = l // 2
            p = h @ ab_w_in[i]
            qa, ka, va, pb = jnp.split(p, [A_WIDTH, 2 * A_WIDTH, 3 * A_WIDTH], axis=-1)
            oa = chunk_attention(qa, ka, va, att_rel_bias[i])
            v_mix = None if i == 0 else (rwkv_v0[i - 1], rwkv_v_down[i - 1], rwkv_v_up[i - 1])
            ob, v_b = rwkv7_group(pb, rwkv_mu[i], rwkv_w0[i], rwkv_w_up[i], rwkv_a0[i], rwkv_a_up[i],
                                  rwkv_g_up[i], rwkv_k_k[i], rwkv_k_a[i], rwkv_r_k[i],
                                  rwkv_ln_w[i], rwkv_ln_b[i], v_first, v_mix)
            if i == 0:
                v_first = v_b
            x = x + jnp.concatenate([oa, ob], axis=-1) @ ab_w_out[i]
        else:
            j = l // 2
            x = x + rglru_block(h, c_w_in[j], c_conv_w[j], c_conv_b[j], c_wa[j], c_ba[j],
                                c_wx[j], c_bx[j], c_lambda[j], c_w_out[j])
        x = x + 0.5 * swiglu(rmsnorm(x, norm_ffn[l, 1]), ffn_w_gate[l, 1], ffn_w_up[l, 1], ffn_w_down[l, 1])
    return rmsnorm(x, norm_final)
```

```python
from contextlib import ExitStack

import numpy as np
import concourse.bass as bass
import concourse.mybir as mybir
from concourse.bass_utils import run_bass_kernel_spmd

F32 = mybir.dt.float32
F32R = mybir.dt.float32r
AF = mybir.ActivationFunctionType
ALU = mybir.AluOpType
AX = mybir.AxisListType

D = 1024
DFF = 2816
NCH = D // 128
NFC = DFF // 128
SEQ = 4096
NT = 2048
NORM_EPS = 1e-6


class Buf:
    __slots__ = ("name", "w", "r")

    def __init__(self, name):
        self.name = name
        self.w = None
        self.r = {}


class Ctx:
    COMPUTE = ("pe", "act", "dve", "pool")

    def __init__(self, nc, stack):
        self.nc = nc
        self.stack = stack
        self.eng = {"pe": nc.tensor, "act": nc.scalar, "dve": nc.vector,
                    "pool": nc.gpsimd, "sp": nc.sync}
        self.sems = {}
        self.cnt = {}
        self.seen = {e: {} for e in self.eng}
        self.ninst = 0
        self.nwaits = 0
        for e in self.COMPUTE:
            self._sem(e)

    def _sem(self, key):
        if key not in self.sems:
            nm = "s_" + "_".join(str(k) for k in (key if isinstance(key, tuple) else (key,)))
            self.sems[key] = self.stack.enter_context(self.nc.semaphore(nm))
            self.cnt[key] = 0
        return self.sems[key]

    def sb(self, name, shape, dt=F32):
        return self.stack.enter_context(self.nc.sbuf_tensor("sb_" + name, shape, dt))

    def ps(self, name, shape, dt=F32):
        return self.stack.enter_context(self.nc.psum_tensor("pp_" + name, shape, dt))

    def _deps(self, engine, reads, writes):
        deps = {}

        def add(tok):
            if tok is None:
                return
            k, v = tok
            if deps.get(k, 0) < v:
                deps[k] = v
        for b in reads:
            add(b.w)
        for b in writes:
            add(b.w)
            for k, v in b.r.items():
                add((k, v))
        out = []
        for k, v in deps.items():
            if k == engine and engine == "pe":
                continue
            if self.seen[engine].get(k, 0) >= v:
                continue
            self.seen[engine][k] = v
            out.append((k, v))
        return out

    def _emit(self, engine, fn, deps):
        e = self.eng[engine]
        for (k, v) in deps[:-1]:
            e.wait_ge(self.sems[k], v)
            self.nwaits += 1
        inst = fn(e)
        if deps:
            k, v = deps[-1]
            inst._wait_ge(self.sems[k], v)
        self.ninst += 1
        return inst

    def _mark(self, tok, reads, writes):
        k, v = tok
        for b in writes:
            b.w = tok
            b.r = {}
        for b in reads:
            if b.r.get(k, 0) < v:
                b.r[k] = v

    def op(self, engine, fn, reads=(), writes=()):
        deps = self._deps(engine, reads, writes)
        inst = self._emit(engine, fn, deps)
        self.cnt[engine] += 1
        inst.then_inc(self.sems[engine], 1)
        tok = (engine, self.cnt[engine])
        self._mark(tok, reads, writes)
        return tok

    def dma(self, queue, out, in_, reads=(), writes=(), key=None):
        key = ("dma", key)
        self._sem(key)
        deps = self._deps(queue, reads, writes)
        if self.cnt[key] > 0 and self.seen[queue].get(key, 0) < self.cnt[key]:
            deps = [d for d in deps if d[0] != key] + [(key, self.cnt[key])]
            self.seen[queue][key] = self.cnt[key]
        inst = self._emit(queue, lambda e: e.dma_start(out=out, in_=in_), deps)
        self.cnt[key] += 16
        inst.then_inc(self.sems[key], 16)
        tok = (key, self.cnt[key])
        self._mark(tok, reads, writes)
        return tok

    def finish(self, engine="sp"):
        e = self.eng[engine]
        for k, s in self.sems.items():
            if isinstance(k, tuple) and self.cnt[k] > 0:
                e.wait_ge(s, self.cnt[k])


def r32(ap):
    return ap.bitcast(F32R)


def build_T(first, mix_cm, n_ffn, emit_h, final, nt=NT, tg=512):
    nc = bass.Bass("TRN2", target_bir_lowering=False)
    ng = nt // tg
    nsub = tg // 128
    dr = {}

    def din(name, shape):
        dr[name] = nc.dram_tensor(name, shape, F32, kind="ExternalInput").ap()
        return dr[name]

    def dout(name, shape):
        dr[name] = nc.dram_tensor(name, shape, F32, kind="ExternalOutput").ap()
        return dr[name]

    x_in = din("x_in", [nt, D] if first else [D, nt])
    ident_d = din("ident", [128, 128])
    if mix_cm:
        mT_d = din("mT", [mix_cm, nt])
        wmo_d = din("w_mo", [mix_cm, D])
    ffn_d = []
    for i in range(n_ffn):
        ffn_d.append((din(f"g{i}", [128, NCH]), din(f"wg{i}", [D, DFF]),
                      din(f"wu{i}", [D, DFF]), din(f"wd{i}", [DFF, D])))
    if emit_h:
        gm_d = din("gm", [128, NCH])
        hT_d = dout("hT", [D, nt])
    if final:
        gf_d = din("gf", [128, NCH])
        y_d = dout("y", [nt, D])
    else:
        xT_d = dout("xT", [D, nt])

    with ExitStack() as st:
        c = Ctx(nc, st)
        ncm = mix_cm // 128 if mix_cm else 0
        ident = c.sb("ident_sb", [128, 128])
        ones = c.sb("ones_sb", [128, 128])
        gains = c.sb("gains", [128, 4, NCH])
        xg = c.sb("xg", [128, NCH, tg])
        hT_ = c.sb("hT_sb", [128, NCH, tg])
        hT = hT_
        aT = c.sb("aT_sb", [128, NFC, tg])
        sq = [c.sb(f"sq{i}", [128, tg]) for i in range(2)]
        rstd = c.sb("rstd", [128, tg])
        tmp = [c.sb(f"tmp{i}", [128, tg]) for i in range(2)]
        wgu = [c.sb(f"wgu{i}", [128, 2, NCH, 128]) for i in range(2)]
        wdn = [c.sb(f"wdn{i}", [128, NFC, 128]) for i in range(2)]
        if first or final:
            tok = [c.sb(f"tok{i}", [128, D]) for i in range(2)]
            b_tok = [Buf(f"tok{i}") for i in range(2)]
        if mix_cm:
            mg = c.sb("mg", [128, ncm, tg])
            wmo = [c.sb(f"wmo{i}", [128, ncm, 128]) for i in range(2)]
            b_mg = Buf("mg")
            b_wmo = [Buf(f"wmo{i}") for i in range(2)]
        ps_a = [c.ps(f"ps_a{i}", [128, tg]) for i in range(2)]
        ps_b = [c.ps(f"ps_b{i}", [128, tg]) for i in range(2)]
        ps_y = [c.ps(f"ps_y{i}", [128, tg]) for i in range(2)]
        ps_s = c.ps("ps_s", [128, tg])

        b_const = Buf("const")
        b_xg = [Buf(f"xg{i}") for i in range(NCH)]
        b_hT = Buf("hT")
        b_aT = [Buf(f"aT{i}") for i in range(NFC)]
        b_sq = [Buf(f"sq{i}") for i in range(2)]
        b_rstd = Buf("rstd")
        b_tmp = [Buf(f"tmp{i}") for i in range(2)]
        b_wgu = [Buf(f"wgu{i}") for i in range(2)]
        b_wdn = [Buf(f"wdn{i}") for i in range(2)]
        b_psa = [Buf(f"psa{i}") for i in range(2)]
        b_psb = [Buf(f"psb{i}") for i in range(2)]
        b_psy = [Buf(f"psy{i}") for i in range(2)]
        b_pss = Buf("pss")
        b_out = Buf("dram_out")

        c.dma("sp", ident[:], ident_d[:, :], writes=[b_const], key="const")
        c.op("pool", lambda e: e.memset(ones[:], 1.0), writes=[b_const])
        gi = 0
        gidx = {}
        for i in range(n_ffn):
            c.dma("sp", gains[:, gi, :], ffn_d[i][0][:, :], writes=[b_const], key="const")
            gidx[f"g{i}"] = gi
            gi += 1
        if emit_h:
            c.dma("sp", gains[:, gi, :], gm_d[:, :], writes=[b_const], key="const")
            gidx["gm"] = gi
            gi += 1
        if final:
            c.dma("sp", gains[:, gi, :], gf_d[:, :], writes=[b_const], key="const")
            gidx["gf"] = gi
            gi += 1

        rr = {"a": 0, "y": 0, "w": 0, "wd": 0, "sq": 0, "tok": 0}

        if final:
            fin = c.sb("fin_sb", [128, NCH, tg])

        def rms_to_hT(gk, t0, rnd=True):
            cast = r32 if rnd else (lambda a: a)
            hT = hT_ if rnd else fin
            for ch in range(NCH):
                s = rr["sq"] % 2
                rr["sq"] += 1
                c.op("act", lambda e: e.activation(out=sq[s][:], in_=xg[:, ch, :], func=AF.Square),
                     reads=[b_xg[ch]], writes=[b_sq[s]])
                c.op("pe", lambda e: e.matmul(ps_s[:, :], ones[:], sq[s][:], start=(ch == 0), stop=(ch == NCH - 1)),
                     reads=[b_sq[s], b_const], writes=[b_pss])
            c.op("dve", lambda e: e.tensor_scalar(rstd[:], ps_s[:, :], 1.0 / D, NORM_EPS, ALU.mult, ALU.add),
                 reads=[b_pss], writes=[b_rstd])
            c.op("act", lambda e: e.activation(out=rstd[:], in_=rstd[:], func=AF.Sqrt),
                 reads=[b_rstd], writes=[b_rstd])
            c.op("dve", lambda e: e.reciprocal(rstd[:], rstd[:]),
                 reads=[b_rstd], writes=[b_rstd])
            for ch in range(NCH):
                c.op("dve", lambda e: e.scalar_tensor_tensor(
                    out=cast(hT[:, ch, :]), in0=xg[:, ch, :], scalar=gains[:, gidx[gk], ch:ch + 1],
                    in1=rstd[:], op0=ALU.mult, op1=ALU.mult),
                    reads=[b_xg[ch], b_rstd, b_const], writes=[b_hT])

        def ffn(i, t0):
            _, wg_d, wu_d, wd_d = ffn_d[i]
            rms_to_hT(f"g{i}", t0)
            wg_v = wg_d.rearrange("(kc p) f -> p kc f", p=128)
            wu_v = wu_d.rearrange("(kc p) f -> p kc f", p=128)
            wd_v = wd_d.rearrange("(fc p) d -> p fc d", p=128)
            for fc in range(NFC):
                s = rr["w"] % 2
                rr["w"] += 1
                c.dma("pool", r32(wgu[s][:, 0, :, :]), wg_v[:, :, fc * 128:(fc + 1) * 128],
                      writes=[b_wgu[s]], key=f"wgu{s}")
                c.dma("pool", r32(wgu[s][:, 1, :, :]), wu_v[:, :, fc * 128:(fc + 1) * 128],
                      writes=[b_wgu[s]], key=f"wgu{s}b")
                p = rr["a"] % 2
                rr["a"] += 1
                for kc in range(NCH):
                    c.op("pe", lambda e: e.matmul(ps_a[p][:, :], r32(wgu[s][:, 0, kc, :]), r32(hT[:, kc, :]),
                                                  start=(kc == 0), stop=(kc == NCH - 1)),
                         reads=[b_wgu[s], b_hT], writes=[b_psa[p]])
                for kc in range(NCH):
                    c.op("pe", lambda e: e.matmul(ps_b[p][:, :], r32(wgu[s][:, 1, kc, :]), r32(hT[:, kc, :]),
                                                  start=(kc == 0), stop=(kc == NCH - 1)),
                         reads=[b_wgu[s], b_hT], writes=[b_psb[p]])
                c.op("act", lambda e: e.activation(out=tmp[p][:], in_=ps_a[p][:, :], func=AF.Silu),
                     reads=[b_psa[p]], writes=[b_tmp[p]])
                c.op("dve", lambda e: e.tensor_tensor(r32(aT[:, fc, :]), tmp[p][:], ps_b[p][:, :], ALU.mult),
                     reads=[b_tmp[p], b_psb[p]], writes=[b_aT[fc]])
            for dc in range(NCH):
                s = rr["wd"] % 2
                rr["wd"] += 1
                c.dma("pool", r32(wdn[s][:, :, :]), wd_v[:, :, dc * 128:(dc + 1) * 128],
                      writes=[b_wdn[s]], key=f"wdn{s}")
                p = rr["y"] % 2
                rr["y"] += 1
                for fc in range(NFC):
                    c.op("pe", lambda e: e.matmul(ps_y[p][:, :], r32(wdn[s][:, fc, :]), r32(aT[:, fc, :]),
                                                  start=(fc == 0), stop=(fc == NFC - 1)),
                         reads=[b_wdn[s], b_aT[fc]], writes=[b_psy[p]])
                c.op("dve", lambda e: e.scalar_tensor_tensor(
                    out=xg[:, dc, :], in0=ps_y[p][:, :], scalar=0.5, in1=xg[:, dc, :],
                    op0=ALU.mult, op1=ALU.add),
                    reads=[b_psy[p], b_xg[dc]], writes=[b_xg[dc]])

        for g in range(ng):
            t0 = g * tg
            if first:
                for sub in range(nsub):
                    s = rr["tok"] % 2
                    rr["tok"] += 1
                    c.dma("sp", tok[s][:], x_in[t0 + sub * 128:t0 + (sub + 1) * 128, :],
                          writes=[b_tok[s]], key=f"tok{s}")
                    for ch in range(NCH):
                        p = rr["y"] % 2
                        rr["y"] += 1
                        c.op("pe", lambda e: e.transpose(ps_y[p][:, 0:128], tok[s][:, ch * 128:(ch + 1) * 128], ident[:]),
                             reads=[b_tok[s], b_const], writes=[b_psy[p]])
                        c.op("act", lambda e: e.copy(out=xg[:, ch, sub * 128:(sub + 1) * 128], in_=ps_y[p][:, 0:128]),
                             reads=[b_psy[p]], writes=[b_xg[ch]])
            else:
                xin_v = x_in.rearrange("(c p) t -> p c t", p=128)
                c.dma("sp", xg[:, :, :], xin_v[:, :, t0:t0 + tg], writes=b_xg, key="xin")
            if mix_cm:
                mT_v = mT_d.rearrange("(c p) t -> p c t", p=128)
                wmo_v = wmo_d.rearrange("(c p) d -> p c d", p=128)
                c.dma("pool", r32(mg[:, :, :]), mT_v[:, :, t0:t0 + tg], writes=[b_mg], key="mg")
                for dc in range(NCH):
                    s = dc % 2
                    c.dma("pool", r32(wmo[s][:, :, :]), wmo_v[:, :, dc * 128:(dc + 1) * 128],
                          writes=[b_wmo[s]], key=f"wmo{s}")
                    p = rr["y"] % 2
                    rr["y"] += 1
                    for cm in range(ncm):
                        c.op("pe", lambda e: e.matmul(ps_y[p][:, :], r32(wmo[s][:, cm, :]), r32(mg[:, cm, :]),
                                                      start=(cm == 0), stop=(cm == ncm - 1)),
                             reads=[b_wmo[s], b_mg], writes=[b_psy[p]])
                    c.op("dve", lambda e: e.tensor_tensor(xg[:, dc, :], ps_y[p][:, :], xg[:, dc, :], ALU.add),
                         reads=[b_psy[p], b_xg[dc]], writes=[b_xg[dc]])
            for i in range(n_ffn):
                ffn(i, t0)
            if emit_h:
                rms_to_hT("gm", t0)
                hT_v = hT_d.rearrange("(c p) t -> p c t", p=128)
                c.dma("sp", hT_v[:, :, t0:t0 + tg], hT[:, :, :], reads=[b_hT], writes=[b_out], key="hout")
            if final:
                rms_to_hT("gf", t0, rnd=False)
                for sub in range(nsub):
                    s = rr["tok"] % 2
                    rr["tok"] += 1
                    for ch in range(NCH):
                        p = rr["y"] % 2
                        rr["y"] += 1
                        c.op("pe", lambda e: e.transpose(ps_y[p][:, 0:128], fin[:, ch, sub * 128:(sub + 1) * 128], ident[:]),
                             reads=[b_hT, b_const], writes=[b_psy[p]])
                        c.op("act", lambda e: e.copy(out=tok[s][:, ch * 128:(ch + 1) * 128], in_=ps_y[p][:, 0:128]),
                             reads=[b_psy[p]], writes=[b_tok[s]])
                    c.dma("sp", y_d[t0 + sub * 128:t0 + (sub + 1) * 128, :], tok[s][:],
                          reads=[b_tok[s]], writes=[b_out], key=f"tok{s}")
            else:
                xT_v = xT_d.rearrange("(c p) t -> p c t", p=128)
                c.dma("sp", xT_v[:, :, t0:t0 + tg], xg[:, :, :], reads=b_xg, writes=[b_out], key="xout")
        c.finish("sp")
    return nc


def build_C(seq=SEQ, nb=5, tg=512):
    nc = bass.Bass("TRN2", target_bir_lowering=False)
    CW = nb * 128

    def din(name, shape):
        return nc.dram_tensor(name, shape, F32, kind="ExternalInput").ap()

    hT_d = din("hT", [D, seq])
    wg_d = din("wg", [D, CW])
    wx_d = din("wx", [D, CW])
    cw_d = din("cw", [128, nb, 4])
    vec_d = din("vecs", [128, nb, 4])
    wa_d = din("wa", [128, nb, 128])
    wxx_d = din("wxx", [128, nb, 128])
    mT_d = nc.dram_tensor("mT", [CW, seq], F32, kind="ExternalOutput").ap()
    ntile = seq // tg

    with ExitStack() as st:
        c = Ctx(nc, st)
        cw = c.sb("cw", [128, nb, 4])
        vec = c.sb("vec", [128, nb, 4])
        c8 = c.sb("c8", [128, nb])
        wa = c.sb("wa", [128, nb, 128])
        wxx = c.sb("wxx", [128, nb, 128])
        wg = c.sb("wg", [128, NCH, 128])
        wx = c.sb("wx", [128, NCH, 128])
        hT = [c.sb(f"hT{i}", [128, NCH, tg]) for i in range(2)]
        xbuf = c.sb("xbuf", [128, tg + 3])
        gx = c.sb("gx", [128, tg])
        t1 = c.sb("t1", [128, tg])
        t2 = c.sb("t2", [128, tg])
        gate = c.sb("gate", [128, tg])
        xc = c.sb("xc", [128, tg])
        rr_ = c.sb("rr", [128, tg])
        ig = c.sb("ig", [128, tg])
        av = c.sb("av", [128, tg])
        bv = c.sb("bv", [128, tg])
        hs = [c.sb(f"hs{i}", [128, tg]) for i in range(2)]
        ob = [c.sb(f"ob{i}", [128, tg]) for i in range(2)]
        ps_g = c.ps("ps_g", [128, tg])
        ps_x = c.ps("ps_x", [128, tg])
        ps_r = c.ps("ps_r", [128, tg])
        ps_i = c.ps("ps_i", [128, tg])
        B = {n: Buf(n) for n in ["const", "wgx", "xbuf", "gx", "t1", "t2", "gate", "xc", "rr", "ig", "av", "bv",
                                 "psg", "psx", "psr", "psi", "out", "c8"]}
        b_hT = [Buf("hT0"), Buf("hT1")]
        b_hs = [Buf("hs0"), Buf("hs1")]
        b_ob = [Buf("ob0"), Buf("ob1")]

        c.dma("sp", cw[:], cw_d[:, :, :], writes=[B["const"]], key="const")
        c.dma("sp", vec[:], vec_d[:, :, :], writes=[B["const"]], key="const")
        c.dma("pool", r32(wa[:]), wa_d[:, :, :], writes=[B["const"]], key="constp")
        c.dma("pool", r32(wxx[:]), wxx_d[:, :, :], writes=[B["const"]], key="constp")
        c.op("act", lambda e: e.activation(out=c8[:], in_=vec[:, :, 3], func=AF.Exp, scale=-1.0),
             reads=[B["const"]], writes=[B["c8"]])
        c.op("dve", lambda e: e.tensor_scalar(c8[:], c8[:], 1.0, None, ALU.add), reads=[B["c8"]], writes=[B["c8"]])
        c.op("act", lambda e: e.activation(out=c8[:], in_=c8[:], func=AF.Ln), reads=[B["c8"]], writes=[B["c8"]])
        c.op("dve", lambda e: e.tensor_scalar(c8[:], c8[:], -8.0, None, ALU.mult), reads=[B["c8"]], writes=[B["c8"]])

        hT_v = hT_d.rearrange("(c p) t -> p c t", p=128)
        wg_v = wg_d.rearrange("(kc p) f -> p kc f", p=128)
        wx_v = wx_d.rearrange("(kc p) f -> p kc f", p=128)
        nld = 0
        for cb in range(nb):
            c.dma("pool", r32(wg[:]), wg_v[:, :, cb * 128:(cb + 1) * 128], writes=[B["wgx"]], key="wgx")
            c.dma("pool", r32(wx[:]), wx_v[:, :, cb * 128:(cb + 1) * 128], writes=[B["wgx"]], key="wgx2")
            c.op("pool", lambda e: e.memset(xbuf[:, 0:3], 0.0), writes=[B["xbuf"]])
            for tt in range(ntile):
                t0 = tt * tg
                s = nld % 2
                nld += 1
                c.dma("pool", r32(hT[s][:]), hT_v[:, :, t0:t0 + tg], writes=[b_hT[s]], key=f"hT{s}")
                for kc in range(NCH):
                    c.op("pe", lambda e: e.matmul(ps_g[:, :], r32(wg[:, kc, :]), r32(hT[s][:, kc, :]),
                                                  start=(kc == 0), stop=(kc == NCH - 1)),
                         reads=[B["wgx"], b_hT[s]], writes=[B["psg"]])
                for kc in range(NCH):
                    c.op("pe", lambda e: e.matmul(ps_x[:, :], r32(wx[:, kc, :]), r32(hT[s][:, kc, :]),
                                                  start=(kc == 0), stop=(kc == NCH - 1)),
                         reads=[B["wgx"], b_hT[s]], writes=[B["psx"]])
                c.op("act", lambda e: e.copy(out=gx[:], in_=ps_g[:, :]), reads=[B["psg"]], writes=[B["gx"]])
                c.op("dve", lambda e: e.tensor_tensor(t1[:], gx[:], gx[:], ALU.mult), reads=[B["gx"]], writes=[B["t1"]])
                c.op("dve", lambda e: e.tensor_scalar(t1[:], t1[:], 0.044715, 1.0, ALU.mult, ALU.add),
                     reads=[B["t1"]], writes=[B["t1"]])
                c.op("dve", lambda e: e.tensor_tensor(t1[:], t1[:], gx[:], ALU.mult), reads=[B["t1"], B["gx"]], writes=[B["t1"]])
                c.op("act", lambda e: e.activation(out=t1[:], in_=t1[:], func=AF.Sigmoid, scale=1.5957691216057308),
                     reads=[B["t1"]], writes=[B["t1"]])
                c.op("dve", lambda e: e.tensor_tensor(gate[:], t1[:], gx[:], ALU.mult), reads=[B["t1"], B["gx"]], writes=[B["gate"]])
                c.op("act", lambda e: e.copy(out=xbuf[:, 3:3 + tg], in_=ps_x[:, :]), reads=[B["psx"]], writes=[B["xbuf"]])
                c.op("dve", lambda e: e.tensor_scalar(t2[:], xbuf[:, 0:tg], cw[:, cb, 0:1], vec[:, cb, 0:1], ALU.mult, ALU.add),
                     reads=[B["xbuf"], B["const"]], writes=[B["t2"]])
                for i in (1, 2):
                    c.op("dve", lambda e: e.scalar_tensor_tensor(out=t2[:], in0=xbuf[:, i:i + tg], scalar=cw[:, cb, i:i + 1],
                                                                 in1=t2[:], op0=ALU.mult, op1=ALU.add),
                         reads=[B["xbuf"], B["t2"], B["const"]], writes=[B["t2"]])
                c.op("dve", lambda e: e.scalar_tensor_tensor(out=r32(xc[:]), in0=xbuf[:, 3:3 + tg], scalar=cw[:, cb, 3:4],
                                                             in1=t2[:], op0=ALU.mult, op1=ALU.add),
                     reads=[B["xbuf"], B["t2"], B["const"]], writes=[B["xc"]])
                c.op("dve", lambda e: e.tensor_copy(xbuf[:, 0:3], xbuf[:, tg:tg + 3]), reads=[B["xbuf"]], writes=[B["xbuf"]])
                c.op("pe", lambda e: e.matmul(ps_r[:, :], r32(wa[:, cb, :]), r32(xc[:]), start=True, stop=True),
                     reads=[B["const"], B["xc"]], writes=[B["psr"]])
                c.op("pe", lambda e: e.matmul(ps_i[:, :], r32(wxx[:, cb, :]), r32(xc[:]), start=True, stop=True),
                     reads=[B["const"], B["xc"]], writes=[B["psi"]])
                c.op("act", lambda e: e.activation(out=rr_[:], in_=ps_r[:, :], func=AF.Sigmoid, bias=vec[:, cb, 1:2]),
                     reads=[B["psr"], B["const"]], writes=[B["rr"]])
                c.op("act", lambda e: e.activation(out=ig[:], in_=ps_i[:, :], func=AF.Sigmoid, bias=vec[:, cb, 2:3]),
                     reads=[B["psi"], B["const"]], writes=[B["ig"]])
                c.op("act", lambda e: e.activation(out=av[:], in_=rr_[:], func=AF.Exp, scale=c8[:, cb:cb + 1]),
                     reads=[B["rr"], B["c8"]], writes=[B["av"]])
                c.op("dve", lambda e: e.tensor_tensor(bv[:], av[:], av[:], ALU.mult), reads=[B["av"]], writes=[B["bv"]])
                c.op("dve", lambda e: e.tensor_scalar(bv[:], bv[:], -1.0, 1.0, ALU.mult, ALU.add), reads=[B["bv"]], writes=[B["bv"]])
                c.op("dve", lambda e: e.tensor_scalar(bv[:], bv[:], 0.0, None, ALU.max), reads=[B["bv"]], writes=[B["bv"]])
                c.op("act", lambda e: e.activation(out=bv[:], in_=bv[:], func=AF.Sqrt), reads=[B["bv"]], writes=[B["bv"]])
                c.op("dve", lambda e: e.tensor_tensor(ig[:], ig[:], xc[:], ALU.mult), reads=[B["ig"], B["xc"]], writes=[B["ig"]])
                c.op("dve", lambda e: e.tensor_tensor(bv[:], bv[:], ig[:], ALU.mult), reads=[B["bv"], B["ig"]], writes=[B["bv"]])
                hp = (tt + 1) % 2
                hc = tt % 2
                init = 0.0 if tt == 0 else hs[hp][:, tg - 1:tg]
                c.op("dve", lambda e: e.tensor_tensor_scan(hs[hc][:], av[:], bv[:], init, ALU.mult, ALU.add),
                     reads=[B["av"], B["bv"], b_hs[hp]], writes=[b_hs[hc]])
                c.op("dve", lambda e: e.tensor_tensor(ob[hc][:], hs[hc][:], gate[:], ALU.mult),
                     reads=[b_hs[hc], B["gate"]], writes=[b_ob[hc]])
                c.dma("sp", mT_d[cb * 128:(cb + 1) * 128, t0:t0 + tg], ob[hc][:], reads=[b_ob[hc]], writes=[B["out"]],
                      key=f"ob{hc}")
        c.finish("sp")
    return nc


NSTEP = 4
DECAY_SCALE = 0.606531
GN_EPS = 64e-5


class _Stop(Exception):
    pass


def build_AB(vres, seq=SEQ, tg=512, stop=9):
    try:
        return _build_AB(vres, seq, tg, stop)
    except _Stop as e:
        return e.args[0]


def _build_AB(vres, seq, tg, stop):
    nc = bass.Bass("TRN2", target_bir_lowering=False)
    nv = 4 if vres else 2
    npc = 6 + nv
    LW, LG = 4 + nv, 5 + nv
    ntile = seq // tg
    nblk = seq // 128

    def din(name, shape):
        return nc.dram_tensor(name, shape, F32, kind="ExternalInput").ap()

    hT_d = din("hT", [D, seq])
    wq_d = din("wq", [D, 256])
    wk_d = din("wk", [D, 256])
    wv_d = din("wv", [D, 256])
    bias_d = din("biasT", [128, 4, 5, 128])
    wp_d = din("wp", [D, npc * 128])
    mu_d = din("mu", [128, npc])
    pv_d = din("pv", [128, 2, 8])
    wa_up_d = din("wa_up", [128, 256])
    g_up_d = din("g_up", [128, 256])
    bones_d = din("bones", [128, 128])
    ident_d = din("ident", [128, 128])
    if vres:
        vdn_d = din("v_down", [128, 4, 32])
        vup_d = din("v_up", [32, 256])
        vf_d = din("vfirst", [256, seq])
    else:
        vfo_d = nc.dram_tensor("vfirst_out", [256, seq], F32, kind="ExternalOutput").ap()
    mT_d = nc.dram_tensor("mT", [512, seq], F32, kind="ExternalOutput").ap()
    scr = nc.dram_tensor("scr", [2, 2, 5, seq, 64], F32).ap()
    gsc = nc.dram_tensor("gsc", [2, 2, 128, seq], F32).ap()

    with ExitStack() as st:
        c = Ctx(nc, st)
        ident = c.sb("ident", [128, 128])
        bones = c.sb("bones", [128, 128])
        ones = c.sb("ones", [128, 128])
        hT = [c.sb(f"hT{i}", [128, NCH, tg]) for i in range(2)]
        b_hT = [Buf("hT0"), Buf("hT1")]
        b_const = Buf("const")
        b_out = Buf("out")
        hT_v = hT_d.rearrange("(c p) t -> p c t", p=128)
        c.dma("sp", ident[:], ident_d[:, :], writes=[b_const], key="const")
        c.dma("sp", bones[:], bones_d[:, :], writes=[b_const], key="const")
        c.op("pool", lambda e: e.memset(ones[:], 1.0), writes=[b_const])
        nld = [0]

        def load_h(t0):
            s = nld[0] % 2
            nld[0] += 1
            c.dma("pool", r32(hT[s][:]), hT_v[:, :, t0:t0 + tg], writes=[b_hT[s]], key=f"hT{s}")
            return s

        def barrier():
            for e in Ctx.COMPUTE + ("sp",):
                for k, s in c.sems.items():
                    if k != e and c.cnt[k] > 0 and c.seen[e].get(k, 0) < c.cnt[k]:
                        c.eng[e].wait_ge(s, c.cnt[k])
                        c.seen[e][k] = c.cnt[k]

        with ExitStack() as ph:
            def sb(name, shape):
                return ph.enter_context(nc.sbuf_tensor("sb_" + name, shape, F32))

            def ps(name, shape):
                return ph.enter_context(nc.psum_tensor("pp_" + name, shape, F32))
            wqk = sb("wqk", [128, 2, NCH, 256])
            wv = sb("wv", [128, NCH, 256])
            biasT = sb("biasT", [128, 4, 5, 128])
            QT = sb("QT", [128, 2, seq])
            KT = sb("KT", [128, 2, seq])
            Vt = sb("Vt", [128, nblk, 256])
            sin = sb("sin", [128, 5, 128])
            ET = sb("ET", [128, 5, 128])
            rden = sb("rden", [128, 128])
            oT = [sb(f"oT{i}", [128, 128]) for i in range(2)]
            ps_p = [ps(f"ps_p{i}", [128, tg]) for i in range(2)]
            ps_s = ps("ps_s", [128, 8, 128])
            ps_o = ps("ps_o", [128, 128])
            ps_d = ps("ps_d", [128, 128])
            b_w, b_QK, b_V, b_sin, b_ET, b_rden = (Buf(n) for n in ["w", "QK", "V", "sin", "ET", "rden"])
            b_oT = [Buf("oT0"), Buf("oT1")]
            b_psp = [Buf("psp0"), Buf("psp1")]
            b_pss, b_pso, b_psd = Buf("pss"), Buf("pso"), Buf("psd")
            c.dma("pool", r32(wqk[:, 0, :, :]), wq_d.rearrange("(kc p) f -> p kc f", p=128), writes=[b_w], key="aw0")
            c.dma("pool", r32(wqk[:, 1, :, :]), wk_d.rearrange("(kc p) f -> p kc f", p=128), writes=[b_w], key="aw1")
            c.dma("pool", r32(wv[:]), wv_d.rearrange("(kc p) f -> p kc f", p=128), writes=[b_w], key="aw2")
            c.dma("sp", biasT[:], bias_d[:, :, :, :], writes=[b_w], key="aw3")
            npp = 0
            for tt in range(ntile):
                t0 = tt * tg
                s = load_h(t0)
                for qk, dst in ((0, QT), (1, KT)):
                    for ch in range(2):
                        p = npp % 2
                        npp += 1
                        for kc in range(NCH):
                            c.op("pe", lambda e: e.matmul(ps_p[p][:, :], r32(wqk[:, qk, kc, ch * 128:(ch + 1) * 128]),
                                                          r32(hT[s][:, kc, :]), start=(kc == 0), stop=(kc == NCH - 1)),
                                 reads=[b_w, b_hT[s]], writes=[b_psp[p]])
                        c.op("act", lambda e: e.copy(out=dst[:, ch, t0:t0 + tg], in_=ps_p[p][:, :]),
                             reads=[b_psp[p]], writes=[b_QK])
                for sub in range(tg // 128):
                    blk = tt * (tg // 128) + sub
                    p = npp % 2
                    npp += 1
                    for kc in range(NCH):
                        c.op("pe", lambda e: e.matmul(ps_p[p][:, 0:256], r32(hT[s][:, kc, sub * 128:(sub + 1) * 128]),
                                                      r32(wv[:, kc, :]), start=(kc == 0), stop=(kc == NCH - 1)),
                             reads=[b_w, b_hT[s]], writes=[b_psp[p]])
                    c.op("act", lambda e: e.copy(out=r32(Vt[:, blk, :]), in_=ps_p[p][:, 0:256]),
                         reads=[b_psp[p]], writes=[b_V])
            for qp in range(nblk):
                k0 = max(0, qp - 4)
                nk = qp - k0 + 1
                r0 = k0 - (qp - 4)
                for ch in range(2):
                    o = oT[(qp * 2 + ch) % 2]
                    bo = b_oT[(qp * 2 + ch) % 2]
                    for h2 in range(2):
                        hh = ch * 2 + h2
                        pl = slice(h2 * 64, (h2 + 1) * 64)
                        for i in range(nk):
                            kt = k0 + i
                            c.op("pe", lambda e: e.matmul(ps_s[:, i, :], KT[pl, ch, kt * 128:(kt + 1) * 128],
                                                          QT[pl, ch, qp * 128:(qp + 1) * 128], start=True, stop=True),
                                 reads=[b_QK], writes=[b_pss])
                        c.op("dve", lambda e: e.scalar_tensor_tensor(out=sin[:, 0:nk, :], in0=ps_s[:, 0:nk, :], scalar=0.125,
                                                                     in1=biasT[:, hh, r0:r0 + nk, :], op0=ALU.mult, op1=ALU.add),
                             reads=[b_pss, b_w], writes=[b_sin])
                        c.op("act", lambda e: e.activation(out=r32(ET[:, 0:nk, :]), in_=sin[:, 0:nk, :], func=AF.Exp),
                             reads=[b_sin], writes=[b_ET])
                        for i in range(nk):
                            kt = k0 + i
                            c.op("pe", lambda e: e.matmul(ps_o[:, :], r32(Vt[:, kt, ch * 128:(ch + 1) * 128]), r32(ET[:, i, :]),
                                                          start=(i == 0), stop=(i == nk - 1)),
                                 reads=[b_V, b_ET], writes=[b_pso])
                        for i in range(nk):
                            c.op("pe", lambda e: e.matmul(ps_d[:, :], ones[:], ET[:, i, :],
                                                          start=(i == 0), stop=(i == nk - 1)),
                                 reads=[b_const, b_ET], writes=[b_psd])
                        c.op("dve", lambda e: e.reciprocal(rden[pl, :], ps_d[pl, :]), reads=[b_psd], writes=[b_rden])
                        c.op("dve", lambda e: e.tensor_tensor(o[pl, :], ps_o[pl, :], rden[pl, :], ALU.mult),
                             reads=[b_pso, b_rden], writes=[bo])
                    c.dma("sp", mT_d[ch * 128:(ch + 1) * 128, qp * 128:(qp + 1) * 128], o[:, :], reads=[bo], writes=[b_out],
                          key=f"oT{(qp * 2 + ch) % 2}")
            barrier()
            if stop == 1:
                c.finish("sp")
                raise _Stop(nc)

        with ExitStack() as ph:
            def sb(name, shape):
                return ph.enter_context(nc.sbuf_tensor("sbB_" + name, shape, F32))

            def ps(name, shape):
                return ph.enter_context(nc.psum_tensor("ppB_" + name, shape, F32))
            wp = sb("wp", [128, NCH, 128])
            mu = sb("mu", [128, npc])
            pv = sb("pv", [128, 2, 8])
            wa_up = sb("wa_up", [128, 256])
            g_up = sb("g_up", [128, 256])
            pbuf = sb("pbuf", [128, npc, tg + 1])
            pst = [sb(f"pst{q}", [128, tg]) for q in range(npc)]
            dtmp = sb("dtmp", [128, tg])
            tw = sb("tw", [128, tg])
            sg = sb("sg", [128, tg])
            X = sb("X", [128, 5, tg])
            av = sb("av", [128, tg])
            t1 = sb("t1", [128, tg])
            t2 = sb("t2", [128, tg])
            gq = sb("gq", [128, 2, tg])
            tm = [sb(f"tm{i}", [128, 5, 128]) for i in range(2)]
            vT = sb("vT", [128, 2, seq])
            oS = sb("oS", [128, 2, seq])
            if vres:
                vdn = sb("vdn", [128, 4, 32])
                vup = sb("vup", [32, 256])
                vd = sb("vd", [32, tg])
                vf = sb("vf", [128, tg])
            ps_p = [ps(f"ps_p{i}", [128, tg]) for i in range(2)]
            ps_l = [ps(f"ps_l{i}", [128, tg]) for i in range(2)]
            ps_t = [ps(f"ps_t{i}", [128, 128]) for i in range(2)]
            B = {n: Buf(n) for n in ["w", "wp", "pbuf", "dtmp", "tw", "sg", "av", "t1", "t2", "gq", "X", "vT", "oS",
                                     "vd", "vf", "scr", "gsc"]}
            b_pst = [Buf(f"pst{i}") for i in range(npc)]
            b_psp = [Buf("psp0"), Buf("psp1")]
            b_psl = [Buf("psl0"), Buf("psl1")]
            b_pst_ = [Buf("pst0"), Buf("pst1")]
            b_tm = [Buf("tm0"), Buf("tm1")]
            c.dma("sp", mu[:], mu_d[:, :], writes=[B["w"]], key="bw0")
            c.dma("sp", pv[:], pv_d[:, :, :], writes=[B["w"]], key="bw0")
            c.dma("pool", r32(wa_up[:]), wa_up_d[:, :], writes=[B["w"]], key="bw1")
            c.dma("pool", r32(g_up[:]), g_up_d[:, :], writes=[B["w"]], key="bw1")
            if vres:
                c.dma("pool", r32(vdn[:]), vdn_d[:, :, :], writes=[B["w"]], key="bw1")
                c.dma("pool", r32(vup[:]), vup_d[:, :], writes=[B["w"]], key="bw1")
            c.op("pool", lambda e: e.memset(pbuf[:, :, 0:1], 0.0), writes=[B["pbuf"]])
            wp_v = wp_d.rearrange("(kc p) f -> p kc f", p=128)
            npp = 0
            nl = 0
            ntm = 0
            for tt in range(ntile):
                t0 = tt * tg
                s = load_h(t0)
                for ch in range(npc):
                    c.dma("pool", r32(wp[:]), wp_v[:, :, ch * 128:(ch + 1) * 128], writes=[B["wp"]], key="wp")
                    p = npp % 2
                    npp += 1
                    for kc in range(NCH):
                        c.op("pe", lambda e: e.matmul(ps_p[p][:, :], r32(wp[:, kc, :]), r32(hT[s][:, kc, :]),
                                                      start=(kc == 0), stop=(kc == NCH - 1)),
                             reads=[B["wp"], b_hT[s]], writes=[b_psp[p]])
                    c.op("act", lambda e: e.copy(out=pbuf[:, ch, 1:tg + 1], in_=ps_p[p][:, :]),
                         reads=[b_psp[p]], writes=[B["pbuf"]])
                    c.op("dve", lambda e: e.tensor_tensor(dtmp[:], pbuf[:, ch, 0:tg], pbuf[:, ch, 1:tg + 1], ALU.subtract),
                         reads=[B["pbuf"]], writes=[B["dtmp"]])
                    isv = vres and 4 <= ch < 4 + nv
                    dst = r32(pst[ch][:, :]) if isv else pst[ch][:, :]
                    c.op("dve", lambda e: e.scalar_tensor_tensor(out=dst, in0=dtmp[:], scalar=mu[:, ch:ch + 1],
                                                                 in1=pbuf[:, ch, 1:tg + 1], op0=ALU.mult, op1=ALU.add),
                         reads=[B["dtmp"], B["pbuf"], B["w"]], writes=[b_pst[ch]])
                    c.op("dve", lambda e: e.tensor_copy(pbuf[:, ch, 0:1], pbuf[:, ch, tg:tg + 1]),
                         reads=[B["pbuf"]], writes=[B["pbuf"]])
                c.op("act", lambda e: e.activation(out=r32(tw[0:64, :]), in_=pst[LW][0:64, :], func=AF.Tanh),
                     reads=[b_pst[LW]], writes=[B["tw"]])
                c.op("act", lambda e: e.copy(out=r32(tw[64:128, :]), in_=pst[LW][64:128, :]),
                     reads=[b_pst[LW]], writes=[B["tw"]])
                c.op("act", lambda e: e.activation(out=r32(sg[:]), in_=pst[LG][:, :], func=AF.Sigmoid),
                     reads=[b_pst[LG]], writes=[B["sg"]])
                if vres:
                    p = nl % 2
                    nl += 1
                    for i in range(4):
                        c.op("pe", lambda e: e.matmul(ps_l[p][0:32, :], r32(vdn[:, i, :]), r32(pst[4 + i][:, :]),
                                                      start=(i == 0), stop=(i == 3)),
                             reads=[B["w"], b_pst[4 + i]], writes=[b_psl[p]])
                    c.op("act", lambda e: e.copy(out=r32(vd[:]), in_=ps_l[p][0:32, :]), reads=[b_psl[p]], writes=[B["vd"]])
                for g in range(2):
                    cs = slice(g * 128, (g + 1) * 128)
                    R_, K_, V_ = pst[g][:, :], pst[2 + g][:, :], pst[4 + g][:, :]
                    bR, bK, bV = b_pst[g], b_pst[2 + g], b_pst[4 + g]
                    if vres:
                        p = nl % 2
                        nl += 1
                        c.op("pe", lambda e: e.matmul(ps_l[p][:, :], r32(vup[:, cs]), r32(vd[:]), start=True, stop=True),
                             reads=[B["w"], B["vd"]], writes=[b_psl[p]])
                        c.op("act", lambda e: e.activation(out=t1[:], in_=ps_l[p][:, :], func=AF.Sigmoid, bias=pv[:, g, 7:8]),
                             reads=[b_psl[p], B["w"]], writes=[B["t1"]])
                        c.dma("sp", vf[:], vf_d[cs, t0:t0 + tg], writes=[B["vf"]], key="vf")
                        c.op("dve", lambda e: e.tensor_tensor(vf[:], vf[:], V_, ALU.subtract), reads=[B["vf"], bV], writes=[B["vf"]])
                        c.op("dve", lambda e: e.tensor_tensor(vf[:], vf[:], t1[:], ALU.mult), reads=[B["vf"], B["t1"]], writes=[B["vf"]])
                        c.op("dve", lambda e: e.tensor_tensor(vT[:, g, t0:t0 + tg], vf[:], V_, ALU.add),
                             reads=[B["vf"], bV], writes=[B["vT"]])
                    else:
                        c.op("act", lambda e: e.copy(out=vT[:, g, t0:t0 + tg], in_=V_), reads=[bV], writes=[B["vT"]])
                        c.dma("sp", vfo_d[cs, t0:t0 + tg], vT[:, g, t0:t0 + tg], reads=[B["vT"]], writes=[b_out], key="vfo")
                    vv = vT[:, g, t0:t0 + tg]
                    p = nl % 2
                    nl += 1
                    c.op("pe", lambda e: e.matmul(ps_l[p][:, :], r32(wa_up[0:64, cs]), r32(tw[0:64, :]), start=True, stop=True),
                         reads=[B["w"], B["tw"]], writes=[b_psl[p]])
                    c.op("act", lambda e: e.activation(out=t1[:], in_=ps_l[p][:, :], func=AF.Sigmoid, bias=pv[:, g, 0:1]),
                         reads=[b_psl[p], B["w"]], writes=[B["t1"]])
                    c.op("act", lambda e: e.activation(out=X[:, 0, :], in_=t1[:], func=AF.Exp, scale=-DECAY_SCALE),
                         reads=[B["t1"]], writes=[B["X"]])
                    p = nl % 2
                    nl += 1
                    c.op("pe", lambda e: e.matmul(ps_l[p][:, :], r32(wa_up[64:128, cs]), r32(tw[64:128, :]), start=True, stop=True),
                         reads=[B["w"], B["tw"]], writes=[b_psl[p]])
                    c.op("act", lambda e: e.activation(out=av[:], in_=ps_l[p][:, :], func=AF.Sigmoid, bias=pv[:, g, 1:2]),
                         reads=[b_psl[p], B["w"]], writes=[B["av"]])
                    p = nl % 2
                    nl += 1
                    c.op("pe", lambda e: e.matmul(ps_l[p][:, :], r32(g_up[:, cs]), r32(sg[:]), start=True, stop=True),
                         reads=[B["w"], B["sg"]], writes=[b_psl[p]])
                    c.op("act", lambda e: e.copy(out=gq[:, 0, :], in_=ps_l[p][:, :]), reads=[b_psl[p]], writes=[B["gq"]])
                    c.op("dve", lambda e: e.tensor_scalar(X[:, 1, :], K_, pv[:, g, 2:3], None, ALU.mult),
                         reads=[bK, B["w"]], writes=[B["X"]])
                    c.op("dve", lambda e: e.tensor_tensor(t1[:], X[:, 1, :], X[:, 1, :], ALU.mult), reads=[B["X"]], writes=[B["t1"]])
                    p = nl % 2
                    nl += 1
                    c.op("pe", lambda e: e.matmul(ps_l[p][:, :], bones[:], t1[:], start=True, stop=True),
                         reads=[b_const, B["t1"]], writes=[b_psl[p]])
                    c.op("dve", lambda e: e.tensor_scalar(t2[:], ps_l[p][:, :], 1e-24, None, ALU.max),
                         reads=[b_psl[p]], writes=[B["t2"]])
                    c.op("act", lambda e: e.activation(out=t2[:], in_=t2[:], func=AF.Sqrt), reads=[B["t2"]], writes=[B["t2"]])
                    c.op("dve", lambda e: e.reciprocal(t2[:], t2[:]), reads=[B["t2"]], writes=[B["t2"]])
                    c.op("dve", lambda e: e.tensor_tensor(X[:, 1, :], X[:, 1, :], t2[:], ALU.mult), reads=[B["X"], B["t2"]], writes=[B["X"]])
                    c.op("dve", lambda e: e.scalar_tensor_tensor(out=X[:, 2, :], in0=X[:, 1, :], scalar=-1.0, in1=av[:],
                                                                 op0=ALU.mult, op1=ALU.mult),
                         reads=[B["X"], B["av"]], writes=[B["X"]])
                    c.op("dve", lambda e: e.tensor_scalar(t1[:], av[:], -1.0, pv[:, g, 3:4], ALU.add, ALU.mult),
                         reads=[B["av"], B["w"]], writes=[B["t1"]])
                    c.op("dve", lambda e: e.scalar_tensor_tensor(out=X[:, 3, :], in0=t1[:], scalar=1.0, in1=K_,
                                                                 op0=ALU.add, op1=ALU.mult),
                         reads=[B["t1"], bK], writes=[B["X"]])
                    c.op("act", lambda e: e.copy(out=X[:, 4, :], in_=R_), reads=[bR], writes=[B["X"]])
                    c.op("dve", lambda e: e.scalar_tensor_tensor(out=t1[:], in0=R_, scalar=pv[:, g, 4:5], in1=X[:, 3, :],
                                                                 op0=ALU.mult, op1=ALU.mult),
                         reads=[bR, B["X"], B["w"]], writes=[B["t1"]])
                    p = nl % 2
                    nl += 1
                    c.op("pe", lambda e: e.matmul(ps_l[p][:, :], bones[:], t1[:], start=True, stop=True),
                         reads=[b_const, B["t1"]], writes=[b_psl[p]])
                    c.op("dve", lambda e: e.tensor_tensor(gq[:, 1, :], ps_l[p][:, :], vv, ALU.mult),
                         reads=[b_psl[p], B["vT"]], writes=[B["gq"]])
                    c.dma("sp", gsc[g, :, :, t0:t0 + tg].rearrange("q p t -> p q t"), gq[:, :, :], reads=[B["gq"]],
                          writes=[B["gsc"]], key="gsc")
                    for sub in range(tg // 128):
                        m = ntm % 2
                        ntm += 1
                        for o in range(5):
                            q = (ntm * 5 + o) % 2
                            c.op("pe", lambda e: e.transpose(ps_t[q][:, :], X[:, o, sub * 128:(sub + 1) * 128], ident[:]),
                                 reads=[B["X"], b_const], writes=[b_pst_[q]])
                            c.op("act", lambda e: e.copy(out=tm[m][:, o, :], in_=ps_t[q][:, :]),
                                 reads=[b_pst_[q]], writes=[b_tm[m]])
                        tb = t0 + sub * 128
                        for h2 in range(2):
                            c.dma("sp", scr[g, h2, :, tb:tb + 128, :].rearrange("o p k -> p o k"),
                                  tm[m][:, :, h2 * 64:(h2 + 1) * 64], reads=[b_tm[m]], writes=[B["scr"]], key=f"tm{m}")
            barrier()
            if stop == 2:
                c.finish("sp")
                raise _Stop(nc)

            with ExitStack() as ph2:
                S = [ph2.enter_context(nc.sbuf_tensor(f"sb_S{g}", [128, 64], F32)) for g in range(2)]
                junk = [ph2.enter_context(nc.sbuf_tensor(f"sb_junk{g}", [128, 64], F32)) for g in range(2)]
                sa = [ph2.enter_context(nc.sbuf_tensor(f"sb_sa{g}", [128, 1], F32)) for g in range(2)]
                bc = [[ph2.enter_context(nc.sbuf_tensor(f"sb_bc{g}_{i}", [128, 5, NSTEP, 64], F32)) for i in range(2)]
                      for g in range(2)]
                b_S = [Buf("S0"), Buf("S1")]
                b_junk = [Buf("j0"), Buf("j1")]
                b_sa = [Buf("sa0"), Buf("sa1")]
                b_bc = [[Buf(f"bc{g}{i}") for i in range(2)] for g in range(2)]
                b_o = [Buf("o0"), Buf("o1")]
                engs = ["dve", "dve"]
                for g in range(2):
                    c.op(engs[g], lambda e: e.memset(S[g][:], 0.0), writes=[b_S[g]])
                for blk in range(seq // NSTEP):
                    tb = blk * NSTEP
                    sl = blk % 2
                    for g in range(2):
                        for h2 in range(2):
                            c.dma("sp", bc[g][sl][h2 * 64:(h2 + 1) * 64, :, :, :],
                                  scr[g, h2:h2 + 1, :, tb:tb + NSTEP, :].partition_broadcast(64) if False else
                                  scr[g, h2, :, tb:tb + NSTEP, :].partition_broadcast(64),
                                  reads=[B["scr"]], writes=[b_bc[g][sl]], key=f"bc{g}{sl}{h2}")
                    for i in range(NSTEP):
                        t = tb + i
                        for g in range(2):
                            en = engs[g]
                            Sg, bcg = S[g], bc[g][sl]
                            rd = [b_bc[g][sl]]
                            c.op(en, lambda e: e.scalar_tensor_tensor(out=junk[g][:], in0=Sg[:], scalar=1.0, in1=bcg[:, 1, i, :],
                                                                      op0=ALU.mult, op1=ALU.mult, accum_out=sa[g][:]),
                                 reads=[b_S[g]] + rd, writes=[b_junk[g], b_sa[g]])
                            c.op(en, lambda e: e.tensor_tensor(Sg[:], Sg[:], bcg[:, 0, i, :], ALU.mult),
                                 reads=[b_S[g]] + rd, writes=[b_S[g]])
                            c.op(en, lambda e: e.scalar_tensor_tensor(out=Sg[:], in0=bcg[:, 2, i, :], scalar=sa[g][:, 0:1], in1=Sg[:],
                                                                      op0=ALU.mult, op1=ALU.add),
                                 reads=[b_S[g], b_sa[g]] + rd, writes=[b_S[g]])
                            c.op(en, lambda e: e.scalar_tensor_tensor(out=Sg[:], in0=bcg[:, 3, i, :], scalar=vT[:, g, t:t + 1], in1=Sg[:],
                                                                      op0=ALU.mult, op1=ALU.add),
                                 reads=[b_S[g], B["vT"]] + rd, writes=[b_S[g]])
                            c.op(en, lambda e: e.scalar_tensor_tensor(out=junk[g][:], in0=Sg[:], scalar=1.0, in1=bcg[:, 4, i, :],
                                                                      op0=ALU.mult, op1=ALU.mult, accum_out=oS[:, g, t:t + 1]),
                                 reads=[b_S[g]] + rd, writes=[b_junk[g], b_o[g]])
                barrier()
                if stop == 3:
                    c.finish("sp")
                    raise _Stop(nc)

            npo = 0
            for tt in range(ntile):
                t0 = tt * tg
                for g in range(2):
                    ov = oS[:, g, t0:t0 + tg]
                    c.dma("sp", gq[:, :, :], gsc[g, :, :, t0:t0 + tg].rearrange("q p t -> p q t"), reads=[B["gsc"]],
                          writes=[B["gq"]], key="gq_in")
                    p = npo % 2
                    npo += 1
                    c.op("pe", lambda e: e.matmul(ps_l[p][:, :], bones[:], ov, start=True, stop=True),
                         reads=[b_const, b_o[g]], writes=[b_psl[p]])
                    c.op("dve", lambda e: e.scalar_tensor_tensor(out=t1[:], in0=ps_l[p][:, :], scalar=-1.0 / 64, in1=ov,
                                                                 op0=ALU.mult, op1=ALU.add),
                         reads=[b_psl[p], b_o[g]], writes=[B["t1"]])
                    c.op("act", lambda e: e.activation(out=t2[:], in_=t1[:], func=AF.Square), reads=[B["t1"]], writes=[B["t2"]])
                    p = npo % 2
                    npo += 1
                    c.op("pe", lambda e: e.matmul(ps_l[p][:, :], bones[:], t2[:], start=True, stop=True),
                         reads=[b_const, B["t2"]], writes=[b_psl[p]])
                    c.op("dve", lambda e: e.tensor_scalar(t2[:], ps_l[p][:, :], 1.0 / 64, GN_EPS, ALU.mult, ALU.add),
                         reads=[b_psl[p]], writes=[B["t2"]])
                    c.op("act", lambda e: e.activation(out=t2[:], in_=t2[:], func=AF.Sqrt), reads=[B["t2"]], writes=[B["t2"]])
                    c.op("dve", lambda e: e.reciprocal(t2[:], t2[:]), reads=[B["t2"]], writes=[B["t2"]])
                    c.op("dve", lambda e: e.tensor_tensor(t1[:], t1[:], t2[:], ALU.mult), reads=[B["t1"], B["t2"]], writes=[B["t1"]])
                    c.op("dve", lambda e: e.tensor_scalar(t1[:], t1[:], pv[:, g, 5:6], pv[:, g, 6:7], ALU.mult, ALU.add),
                         reads=[B["t1"], B["w"]], writes=[B["t1"]])
                    c.op("dve", lambda e: e.tensor_tensor(t1[:], t1[:], gq[:, 1, :], ALU.add), reads=[B["t1"], B["gq"]], writes=[B["t1"]])
                    c.op("dve", lambda e: e.tensor_tensor(dtmp[:], t1[:], gq[:, 0, :], ALU.mult),
                         reads=[B["t1"], B["gq"]], writes=[B["dtmp"]])
                    c.dma("sp", mT_d[256 + g * 128:256 + (g + 1) * 128, t0:t0 + tg], dtmp[:], reads=[B["dtmp"]],
                          writes=[b_out], key="mout")
        c.finish("sp")
    return nc


def _c(a):
    return np.ascontiguousarray(a, dtype=np.float32)


def _pcol(v):
    return _c(np.asarray(v).reshape(-1, 128).T)


_IDENT = np.eye(128, dtype=np.float32)
_BONES = np.kron(np.eye(2, dtype=np.float32), np.ones((64, 64), dtype=np.float32))


def prep_AB(inp, i, j, vfirst=None):
    w_in = np.asarray(inp["ab_w_in"][i])
    own = np.arange(j * 256, (j + 1) * 256)
    oth = np.arange((1 - j) * 256, (2 - j) * 256)
    vres = i > 0
    base = 1536
    cols = np.concatenate([base + own, base + 512 + own, base + 1024 + own]
                          + ([base + 1024 + oth] if vres else [])
                          + [base + 1536 + np.arange(256)])
    pbcols = cols - base
    rb = np.asarray(inp["att_rel_bias"][i])[j * 4:(j + 1) * 4]
    ki = np.arange(128)[:, None, None]
    r = np.arange(5)[None, :, None]
    qi = np.arange(128)[None, None, :]
    idx = np.clip(512 - 128 * r + qi - ki, -128, 128) + 128
    bias = rb[:, idx]
    masked = ((r == 0) & (ki < 64) & (qi >= 64)) | ((r == 4) & (ki >= 64) & (qi < 64))
    bias = np.where(masked[None], np.float32(-30000.0), bias)
    zero = np.zeros(512, np.float32)
    v0 = np.asarray(inp["rwkv_v0"][i - 1]) if vres else zero
    vecs = [inp["rwkv_w0"][i], inp["rwkv_a0"][i], inp["rwkv_k_k"][i], inp["rwkv_k_a"][i],
            np.asarray(inp["rwkv_r_k"][i]).reshape(-1), inp["rwkv_ln_w"][i], inp["rwkv_ln_b"][i], v0]
    pv = np.stack([np.asarray(v)[own].reshape(2, 128).T for v in vecs], axis=-1)
    d = {
        "wq": _c(w_in[:, own]), "wk": _c(w_in[:, 512 + own]), "wv": _c(w_in[:, 1024 + own]),
        "biasT": _c(bias.transpose(1, 0, 2, 3)),
        "wp": _c(w_in[:, cols]),
        "mu": _pcol(np.asarray(inp["rwkv_mu"][i])[pbcols]),
        "pv": _c(pv),
        "wa_up": _c(np.concatenate([np.asarray(inp["rwkv_w_up"][i])[:, own], np.asarray(inp["rwkv_a_up"][i])[:, own]], axis=0)),
        "g_up": _c(np.asarray(inp["rwkv_g_up"][i])[:, own]),
        "bones": _BONES, "ident": _IDENT,
    }
    if vres:
        vdn = np.asarray(inp["rwkv_v_down"][i - 1])[np.concatenate([own, oth])]
        d["v_down"] = _c(vdn.reshape(4, 128, 32).transpose(1, 0, 2))
        d["v_up"] = _c(np.asarray(inp["rwkv_v_up"][i - 1])[:, own])
        d["vfirst"] = _c(vfirst)
    return d


def prep_C(inp, jl, j):
    ch = np.arange(j * 640, (j + 1) * 640)
    w_in = np.asarray(inp["c_w_in"][jl])
    vec = np.stack([_pcol(np.asarray(inp[k][jl])[ch]) for k in ("c_conv_b", "c_ba", "c_bx", "c_lambda")], axis=-1)
    return {
        "wg": _c(w_in[:, ch]), "wx": _c(w_in[:, 1280 + ch]),
        "cw": _c(np.asarray(inp["c_conv_w"][jl])[:, ch].reshape(4, 5, 128).transpose(2, 1, 0)),
        "vecs": _c(vec),
        "wa": _c(np.asarray(inp["c_wa"][jl])[j * 5:(j + 1) * 5].transpose(1, 0, 2)),
        "wxx": _c(np.asarray(inp["c_wx"][jl])[j * 5:(j + 1) * 5].transpose(1, 0, 2)),
    }


_CACHE = {}


def _get(key, fn):
    if key not in _CACHE:
        _CACHE[key] = fn()
    return _CACHE[key]


def _run(nc, in_maps):
    res = run_bass_kernel_spmd(nc, in_maps, core_ids=list(range(8)))
    return res.results


def _ffn_inputs(inp, l, k, idx):
    return {f"g{idx}": _pcol(inp["norm_ffn"][l, k]), f"wg{idx}": _c(inp["ffn_w_gate"][l, k]),
            f"wu{idx}": _c(inp["ffn_w_up"][l, k]), f"wd{idx}": _c(inp["ffn_w_down"][l, k])}


def kernel(**inp):
    inp = {k: np.asarray(v) for k, v in inp.items()}
    x = inp["x"]
    depth = 4
    cores = [(b, j) for b in range(4) for j in range(2)]
    nc = _get(("T", True, 0, 1, True, False), lambda: build_T(True, 0, 1, True, False))
    maps = []
    for (b, j) in cores:
        d = {"x_in": _c(x[b, j * NT:(j + 1) * NT, :]), "ident": _IDENT, "gm": _pcol(inp["norm_mix"][0])}
        d.update(_ffn_inputs(inp, 0, 0, 0))
        maps.append(d)
    res = _run(nc, maps)
    xT = [r["xT"] for r in res]
    hT = [r["hT"] for r in res]
    vfirst = [None] * 8
    out = None
    for l in range(depth):
        hfull = [np.concatenate([hT[2 * b], hT[2 * b + 1]], axis=1) for b in range(4)]
        if l % 2 == 0:
            i = l // 2
            nc = _get(("AB", i > 0), lambda: build_AB(i > 0))
            maps = []
            for ci, (b, j) in enumerate(cores):
                d = prep_AB(inp, i, j, vfirst=vfirst[ci])
                d["hT"] = _c(hfull[b])
                maps.append(d)
            res = _run(nc, maps)
            if i == 0:
                vfirst = [r["vfirst_out"] for r in res]
            mfull = []
            for b in range(4):
                m0, m1 = res[2 * b]["mT"], res[2 * b + 1]["mT"]
                mfull.append(np.concatenate([m0[0:256], m1[0:256], m0[256:512], m1[256:512]], axis=0))
            w_mo = _c(inp["ab_w_out"][i])
            cm = 1024
        else:
            jl = l // 2
            nc = _get(("C",), lambda: build_C())
            maps = []
            for (b, j) in cores:
                d = prep_C(inp, jl, j)
                d["hT"] = _c(hfull[b])
                maps.append(d)
            res = _run(nc, maps)
            mfull = [np.concatenate([res[2 * b]["mT"], res[2 * b + 1]["mT"]], axis=0) for b in range(4)]
            w_mo = _c(inp["c_w_out"][jl])
            cm = 1280
        last = l == depth - 1
        if last:
            nc = _get(("T", False, cm, 1, False, True), lambda: build_T(False, cm, 1, False, True))
        else:
            nc = _get(("T", False, cm, 2, True, False), lambda: build_T(False, cm, 2, True, False))
        maps = []
        for ci, (b, j) in enumerate(cores):
            d = {"x_in": xT[ci], "ident": _IDENT, "mT": _c(mfull[b][:, j * NT:(j + 1) * NT]), "w_mo": w_mo}
            d.update(_ffn_inputs(inp, l, 1, 0))
            if last:
                d["gf"] = _pcol(inp["norm_final"])
            else:
                d.update(_ffn_inputs(inp, l + 1, 0, 1))
                d["gm"] = _pcol(inp["norm_mix"][l + 1])
            maps.append(d)
        res = _run(nc, maps)
        if last:
            out = np.stack([np.concatenate([res[2 * b]["y"], res[2 * b + 1]["y"]], axis=0) for b in range(4)], axis=0)
        else:
            xT = [r["xT"] for r in res]
            hT = [r["hT"] for r in res]
    return out.astype(np.float32)
```
